# Optimizing a Trainium2 kernel written in Bass

```python
import math
import jax
import jax.numpy as jnp
from jax import lax
import numpy as np

D_MODEL = 2048
BATCH = 4
SEQ = 2048
DEPTH = 2

D_FF = 5632
RG_HEADS = 8
RG_DK = 128
RG_DV = 128
RG_WIDTH = RG_HEADS * RG_DV
CHUNK = 64
ATT_HEADS = 8
ATT_KV_HEADS = 2
ATT_GROUP = ATT_HEADS // ATT_KV_HEADS
ATT_HD = 128
ATT_WIDTH = ATT_HEADS * ATT_HD
KV_WIDTH = ATT_KV_HEADS * ATT_HD
WINDOW = 128
Q_BLOCK = 128
IN_SIZES = (RG_WIDTH, RG_WIDTH, RG_WIDTH, RG_WIDTH, RG_WIDTH, ATT_WIDTH, KV_WIDTH, KV_WIDTH, D_MODEL, D_MODEL)
IN_COLS = 5 * RG_WIDTH + ATT_WIDTH + 2 * KV_WIDTH + 2 * D_MODEL
EPS = 1e-6

kernel_name = "hybrid_hgrn2_window_gqa_macaron_encoder"


def rmsnorm(x, gain):
    xf = x.astype(jnp.float32)
    y = xf * lax.rsqrt(jnp.mean(xf * xf, axis=-1, keepdims=True) + EPS)
    return (y * gain.astype(jnp.float32)).astype(x.dtype)


def swiglu(h, w_gate, w_up, w_down):
    return (jax.nn.silu(h @ w_gate) * (h @ w_up)) @ w_down


def lower_bounds(lb_logits):
    lb = jnp.cumsum(jax.nn.softmax(lb_logits.astype(jnp.float32), axis=0), axis=0)
    return lb - lb[0:1]


def hgrn2_direction(q, k, v, log_f):
    B, S, H, DK = q.shape
    DV = v.shape[-1]
    n = S // CHUNK

    def to_chunks(t):
        return t.reshape(B, n, CHUNK, H, t.shape[-1]).transpose(1, 0, 3, 2, 4)

    qc, kc, vc, ac = to_chunks(q), to_chunks(k), to_chunks(v), to_chunks(log_f)
    causal_in_chunk = jnp.tril(jnp.ones((CHUNK, CHUNK), dtype=bool))[:, :, None]

    def step(state, inp):
        qi, ki, vi, ai = inp
        b = jnp.cumsum(ai, axis=2)
        rel = b[:, :, :, None, :] - b[:, :, None, :, :]
        decay = jnp.exp(jnp.where(causal_in_chunk, rel, -jnp.inf))
        scores = jnp.einsum('bhtd,bhsd,bhtsd->bhts', qi, ki, decay)
        o_intra = jnp.einsum('bhts,bhsv->bhtv', scores, vi)
        o_inter = jnp.einsum('bhtd,bhdv->bhtv', qi * jnp.exp(b), state)
        b_last = b[:, :, -1:, :]
        k_dec = ki * jnp.exp(b_last - b)
        new_state = jnp.exp(b_last[:, :, 0, :])[..., None] * state + jnp.einsum('bhsd,bhsv->bhdv', k_dec, vi)
        return new_state, o_intra + o_inter

    state0 = jnp.zeros((B, H, DK, DV), jnp.float32)
    _, o = lax.scan(step, state0, (qc, kc, vc, ac))
    return o.transpose(1, 0, 3, 2, 4).reshape(B, S, H, DV)


def hgrn2_mixer(q, i, zf, zb, g, lb_f, lb_b, out_gain):
    B, S, _ = q.shape
    dt = q.dtype

    def heads(t):
        return t.astype(jnp.float32).reshape(B, S, RG_HEADS, -1)

    qh, vh = heads(q), heads(i)

    def gates(z, lb):
        z = heads(z)
        lb = lb.astype(jnp.float32).reshape(RG_HEADS, RG_DK)
        log_f = jnp.logaddexp(jnp.log(lb), jnp.log1p(-lb) + jax.nn.log_sigmoid(z))
        key = (1.0 - lb) * jax.nn.sigmoid(-z)
        return key, log_f

    k_f, a_f = gates(zf, lb_f)
    k_b, a_b = gates(zb, lb_b)
    o_fwd = hgrn2_direction(qh, k_f, vh, a_f)
    flip = lambda t: jnp.flip(t, axis=1)
    o_bwd = flip(hgrn2_direction(flip(qh), flip(k_b), flip(vh), flip(a_b)))
    o = o_fwd + o_bwd
    o = o * lax.rsqrt(jnp.mean(o * o, axis=-1, keepdims=True) + EPS)
    o = o * out_gain.astype(jnp.float32).reshape(RG_HEADS, RG_DV)
    o = o.reshape(B, S, RG_WIDTH) * jax.nn.silu(g.astype(jnp.float32))
    return o.astype(dt)


def window_gqa(q, k, v, sink):
    B, S, _ = q.shape
    dt = q.dtype
    nb = S // Q_BLOCK
    qb = q.reshape(B, nb, Q_BLOCK, ATT_KV_HEADS, ATT_GROUP, ATT_HD)

    def band(t):
        tp = jnp.pad(t.reshape(B, S, ATT_KV_HEADS, ATT_HD), ((0, 0), (Q_BLOCK, Q_BLOCK), (0, 0), (0, 0)))
        tb = tp.reshape(B, nb + 2, Q_BLOCK, ATT_KV_HEADS, ATT_HD)
        return jnp.concatenate([tb[:, :-2], tb[:, 1:-1], tb[:, 2:]], axis=2)

    kb, vb = band(k), band(v)
    qi = jnp.arange(Q_BLOCK)[:, None]
    kj = jnp.arange(3 * Q_BLOCK)[None, :]
    rel = kj - Q_BLOCK - qi
    kpos = jnp.arange(nb)[:, None, None] * Q_BLOCK + kj[None] - Q_BLOCK
    valid = (jnp.abs(rel) <= WINDOW)[None] & (kpos >= 0) & (kpos < S)

    slopes = 2.0 ** (-8.0 * jnp.arange(1, ATT_HEADS + 1, dtype=jnp.float32) / ATT_HEADS)
    slopes = slopes.reshape(ATT_KV_HEADS, ATT_GROUP)
    dist = jnp.abs(rel).astype(jnp.float32)

    s = jnp.einsum('bnqhgd,bnkhd->bnhgqk', qb, kb).astype(jnp.float32) * (1.0 / math.sqrt(ATT_HD))
    s = s - slopes[:, :, None, None] * dist
    s = jnp.where(valid[None, :, None, None], s, -jnp.inf)
    sink_logit = jnp.broadcast_to(sink.astype(jnp.float32).reshape(ATT_KV_HEADS, ATT_GROUP)[:, :, None, None],
                                  (B, nb, ATT_KV_HEADS, ATT_GROUP, Q_BLOCK, 1))
    p = jax.nn.softmax(jnp.concatenate([s, sink_logit], axis=-1), axis=-1)[..., :-1]
    o = jnp.einsum('bnhgqk,bnkhd->bnqhgd', p.astype(dt), vb)
    return o.reshape(B, S, ATT_WIDTH)


def setup_inputs(seed: int = 0) -> dict:
    key = jax.random.key(seed)
    ks = jax.random.split(key, 20)

    def w(k, shape, fan_in):
        return jax.random.normal(k, shape, jnp.float32) * (fan_in ** -0.5)

    def gain(k, shape):
        return 1.0 + 0.02 * jax.random.normal(k, shape, jnp.float32)

    return {
        "x": jax.random.normal(ks[0], (BATCH, SEQ, D_MODEL), jnp.float32),
        "ffn1_norm": gain(ks[1], (DEPTH, D_MODEL)),
        "ffn1_w_gate": w(ks[2], (DEPTH, D_MODEL, D_FF), D_MODEL),
        "ffn1_w_up": w(ks[3], (DEPTH, D_MODEL, D_FF), D_MODEL),
        "ffn1_w_down": w(ks[4], (DEPTH, D_FF, D_MODEL), D_FF),
        "mix_norm": gain(ks[5], (DEPTH, D_MODEL)),
        "w_in": w(ks[6], (DEPTH, D_MODEL, IN_COLS), D_MODEL),
        "lb_fwd_logits": 0.5 * jax.random.normal(ks[7], (DEPTH, RG_WIDTH), jnp.float32),
        "lb_bwd_logits": 0.5 * jax.random.normal(ks[8], (DEPTH, RG_WIDTH), jnp.float32),
        "rg_out_norm": gain(ks[9], (DEPTH, RG_WIDTH)),
        "attn_sink": 0.5 * jax.random.normal(ks[10], (DEPTH, ATT_HEADS), jnp.float32),
        "w_branch_a": w(ks[11], (DEPTH, RG_WIDTH, D_MODEL), RG_WIDTH),
        "w_branch_b": w(ks[12], (DEPTH, ATT_WIDTH, D_MODEL), ATT_WIDTH),
        "w_out": w(ks[13], (DEPTH, D_MODEL, D_MODEL), D_MODEL),
        "ffn2_norm": gain(ks[14], (DEPTH, D_MODEL)),
        "ffn2_w_gate": w(ks[15], (DEPTH, D_MODEL, D_FF), D_MODEL),
        "ffn2_w_up": w(ks[16], (DEPTH, D_MODEL, D_FF), D_MODEL),
        "ffn2_w_down": w(ks[17], (DEPTH, D_FF, D_MODEL), D_FF),
        "final_norm": gain(ks[18], (D_MODEL,)),
    }


def reference(x, ffn1_norm, ffn1_w_gate, ffn1_w_up, ffn1_w_down, mix_norm, w_in, lb_fwd_logits,
              lb_bwd_logits, rg_out_norm, attn_sink, w_branch_a, w_branch_b, w_out, ffn2_norm,
              ffn2_w_gate, ffn2_w_up, ffn2_w_down, final_norm):
    split_at = [int(c) for c in np.cumsum(IN_SIZES)[:-1]]
    lb_f_all = lower_bounds(lb_fwd_logits)
    lb_b_all = lower_bounds(lb_bwd_logits)
    for l in range(DEPTH):
        h = rmsnorm(x, ffn1_norm[l])
        x = x + 0.5 * swiglu(h, ffn1_w_gate[l], ffn1_w_up[l], ffn1_w_down[l])
        h = rmsnorm(x, mix_norm[l])
        p = h @ w_in[l]
        rq, ri, rzf, rzb, rg, aq, ak, av, ga, gb = jnp.split(p, split_at, axis=-1)
        o_a = hgrn2_mixer(rq, ri, rzf, rzb, rg, lb_f_all[l], lb_b_all[l], rg_out_norm[l])
        o_b = window_gqa(aq, ak, av, attn_sink[l])
        merged = jax.nn.sigmoid(ga) * (o_a @ w_branch_a[l]) + jax.nn.sigmoid(gb) * (o_b @ w_branch_b[l])
        x = x + merged @ w_out[l]
        h = rmsnorm(x, ffn2_norm[l])
        x = x + 0.5 * swiglu(h, ffn2_w_gate[l], ffn2_w_up[l], ffn2_w_down[l])
    return rmsnorm(x, final_norm)
```

```python
import contextlib
import math
import numpy as np
import concourse.bass as bass
import concourse.mybir as mybir
from concourse.bass_utils import run_bass_kernel_spmd

F32 = mybir.dt.float32
BF16 = mybir.dt.bfloat16
AF = mybir.ActivationFunctionType
ALU = mybir.AluOpType
AX = mybir.AxisListType

L = 2
D = 2048
DFF = 5632
NFC = DFF // 128
T = 1024
NT = T // 128
RGW = 1024
INC = 10752
NJ = INC // 128
EPS = 1e-6
GRAN = 1024
COMPUTE = ("pe", "act", "dve", "pool")


class Op:
    __slots__ = ("eng", "fn", "deps", "tok", "signal", "dma_key", "dma_val", "idx", "inc")


class Prog:
    def __init__(self, nc):
        self.nc = nc
        self.ops = {e: [] for e in ("pe", "act", "dve", "pool", "sp")}
        self.last_write = {}
        self.readers = {}
        self.dma_total = {}
        self.order = []

    def _add(self, eng, fn, reads, writes, dma_key=None, inc=16):
        op = Op()
        op.eng = eng
        op.fn = fn
        op.signal = False
        op.dma_key = dma_key
        op.idx = len(self.ops[eng])
        is_dma = dma_key is not None
        if is_dma:
            self.dma_total[dma_key] = self.dma_total.get(dma_key, 0) + inc
            op.inc = inc
            op.dma_val = self.dma_total[dma_key]
            op.tok = ("dma", dma_key, op.dma_val)
        else:
            op.tok = ("eng", eng, op.idx)
        deps = set()
        raw_src = set()
        for r in reads:
            lw = self.last_write.get(r)
            if lw is not None:
                deps.add(lw)
                raw_src.add(lw)
        for w in writes:
            lw = self.last_write.get(w)
            if lw is not None:
                deps.add(lw)
            rd = self.readers.get(w)
            if rd:
                deps.update(rd.values())
        final = set()
        own = inc if is_dma else 0
        deps = set((("dma", t[1], self.dma_total[t[1]] - (own if t[1] == dma_key else 0)) if t[0] == "dma" else t) for t in deps)
        for t in deps:
            if t[0] == "eng" and t[1] == eng and not is_dma:
                if eng != "pe":
                    final.add(t)
                continue
            final.add(t)
        op.deps = final
        tok = op.tok
        rk = (tok[0], tok[1])
        for r in reads:
            self.readers.setdefault(r, {})[rk] = tok
        for w in writes:
            self.last_write[w] = tok
            self.readers[w] = {}
        self.ops[eng].append(op)
        self.order.append(op)
        return op

    def op(self, eng, fn, reads=(), writes=()):
        return self._add(eng, fn, tuple(reads), tuple(writes))

    def dma(self, queue, fn, reads=(), writes=(), key=None, inc=16):
        writes = tuple(writes)
        return self._add(queue, fn, tuple(reads), writes, dma_key=key, inc=inc)

    def emit(self, final_dma_keys=()):
        nc = self.nc
        by_tok = {}
        for e, lst in self.ops.items():
            for o in lst:
                by_tok[o.tok] = o
        for o in self.order:
            for t in o.deps:
                if t[0] == "eng":
                    by_tok[t].signal = True
        cnt_of = {}
        for e in COMPUTE:
            c = 0
            for o in self.ops[e]:
                if o.dma_key is None and o.signal:
                    c += 1
                    cnt_of[o.tok] = c
        dma_keys = sorted(self.dma_total.keys(), key=str)
        with contextlib.ExitStack() as st:
            esem = {e: st.enter_context(nc.semaphore("s_" + e)) for e in COMPUTE}
            dsem = {k: st.enter_context(nc.semaphore("d%d" % i)) for i, k in enumerate(dma_keys)}
            block = st.enter_context(nc.Block())
            engobj = {"pe": "tensor", "act": "scalar", "dve": "vector", "pool": "gpsimd", "sp": "sync"}

            def make(e):
                def body(eng):
                    known = {}
                    for o in self.ops[e]:
                        need = {}
                        for t in o.deps:
                            if t[0] == "eng":
                                s, v = esem[t[1]], cnt_of[t]
                            else:
                                s, v = dsem[t[1]], t[2]
                            k = id(s)
                            if need.get(k, (None, 0))[1] < v:
                                need[k] = (s, v)
                        for k, (s, v) in need.items():
                            if known.get(k, 0) >= v:
                                continue
                            eng.wait_ge(s, v)
                            known[k] = v
                        ins = o.fn(eng)
                        if o.dma_key is not None:
                            ins.then_inc(dsem[o.dma_key], o.inc)
                        elif o.signal:
                            ins.then_inc(esem[e], 1)
                    if e == "sp":
                        for k in final_dma_keys:
                            eng.wait_ge(dsem[k], self.dma_total[k])
                return body

            for e in ("pe", "act", "dve", "pool", "sp"):
                getattr(block, engobj[e])(make(e))


class Buf:
    def __init__(self, arena, off, nbytes, dtype, shape=None, name=None):
        assert off % 4 == 0
        self.off = off
        self.nbytes = nbytes
        v = arena[:, off // 2:(off + nbytes) // 2]
        if dtype == F32:
            v = v.bitcast(F32)
        self.flat = v
        if shape is not None and len(shape) == 2:
            v = v.rearrange("p (a b) -> p a b", b=shape[1])
        elif shape is not None and len(shape) == 3:
            v = v.rearrange("p (a b c) -> p a b c", b=shape[1], c=shape[2])
        self.ap = v
        self.keys = tuple(("g", j) for j in range(off // GRAN, (off + nbytes + GRAN - 1) // GRAN))
        self.name = name or ("b%d" % off)

    def sub(self, lo, hi):
        a = self.off + lo
        b = self.off + hi
        return tuple(("g", j) for j in range(a // GRAN, (b + GRAN - 1) // GRAN))


def _consts():
    r = np.arange(128)
    ch = r // 64
    same = ch[:, None] == ch[None, :]
    M2f = (same & (r[:, None] <= r[None, :])).astype(np.float32)
    M2b = (same & (r[:, None] >= r[None, :])).astype(np.float32)
    mid = ch * 64 + 31
    M1f = M2f - M2f[:, mid]
    M1b = M2b - M2b[:, mid]
    M3f = (same & (r[:, None] > r[None, :])).astype(np.float32)
    M3b = (same & (r[:, None] < r[None, :])).astype(np.float32)
    UF = (r[:, None] <= r[None, :]).astype(np.float32)
    UB = (r[:, None] >= r[None, :]).astype(np.float32)
    ONES = np.ones((128, 128), np.float32)
    IDN = np.eye(128, dtype=np.float32)
    mats = [M1f, M2f, M3f, UF, M1b, M2b, M3b, UB, ONES, IDN]
    q = np.arange(128)[:, None]
    kj = np.arange(384)[None, :]
    rel = kj - 128 - q
    dist = np.abs(rel).astype(np.float32)
    slopes = 2.0 ** (-8.0 * np.arange(1, 9, dtype=np.float32) / 8)
    bias = np.where(np.abs(rel) <= 128, 0.0, -30000.0)[None].astype(np.float32) - slopes[:, None, None] * dist[None]
    bias = np.where(np.abs(rel)[None] <= 128, bias, -30000.0).astype(np.float32)
    bias = np.transpose(bias, (1, 0, 2)).reshape(128, 8 * 384)
    rowA = (r < 64).astype(np.float32)[:, None]
    rowB = (r >= 64).astype(np.float32)[:, None]
    cst = np.concatenate(mats + [rowA, rowB], axis=1).astype(np.float32)
    return cst, np.ascontiguousarray(bias)


NMAT = 10
C_ROW = NMAT * 128
NCST = C_ROW + 2


def build(nl=L, do_ffn=True, do_mix=True, do_hgrn=True, do_att=True, nfc=NFC, dbg=False, lbase=0):
    nc = bass.Bass("TRN2", target_bir_lowering=False)
    dt = nc.dram_tensor
    x_d = dt("x", [T, D], F32, kind="ExternalInput").ap()
    wgu_d = dt("wgu", [nl, 2, nfc, 2, 128, D], F32, kind="ExternalInput").ap()
    wd_d = dt("wd", [nl, 2, nfc, 128, D], F32, kind="ExternalInput").ap()
    win_d = dt("win", [nl if do_mix else 1, NJ if do_mix else 1, 128, D], F32, kind="ExternalInput").ap()
    wab_d = dt("wab", [nl if do_mix else 1, 2, 16 if do_mix else 1, 128, RGW], F32, kind="ExternalInput").ap()
    wout_d = dt("wout", [nl if do_mix else 1, 4 if do_mix else 1, 4 if do_mix else 1, 128, D], F32, kind="ExternalInput").ap()
    gn_d = dt("gn", [3 * L + 1, D], F32, kind="ExternalInput").ap()
    lbl_d = dt("lbl", [2, L, 8, 1024], F32, kind="ExternalInput").ap()
    rgn_d = dt("rgn", [128, L * 8], F32, kind="ExternalInput").ap()
    sink_d = dt("sink", [128, L * 8], F32, kind="ExternalInput").ap()
    cst_d = dt("cst", [128, NCST], F32, kind="ExternalInput").ap()
    flag_d = dt("flag", [128, 2], F32, kind="ExternalInput").ap()
    abias_d = dt("abias", [128, 8 * 384], F32, kind="ExternalInput").ap()
    y_d = dt("y", [T, D], F32, kind="ExternalOutput").ap()
    if dbg:
        dbg_d = dt("dbg", [2, 8, 128, T], BF16, kind="ExternalOutput").ap()
    kv_in = dt("kv_in", [128, 1024], BF16).ap()
    kv_out = dt("kv_out", [2 * 128, 1024], BF16).ap()
    st_in = [dt("st_in%d" % i, [128, 256], F32).ap() for i in range(2)]
    st_out = [dt("st_out%d" % i, [2 * 128, 256], F32).ap() for i in range(2)]

    st = contextlib.ExitStack()
    with st:
        ARENA_BYTES = 196 * 1024
        arena = st.enter_context(nc.sbuf_tensor("arena", [128, ARENA_BYTES // 2], BF16))
        cst = st.enter_context(nc.sbuf_tensor("cstt", [128, NCST], BF16))
        rgnc = st.enter_context(nc.sbuf_tensor("rgnc", [128, L * 8], F32))
        Sf = st.enter_context(nc.sbuf_tensor("Sf", [128, 128], F32))
        Sb = st.enter_context(nc.sbuf_tensor("Sb", [128, 128], BF16))
        scm = [st.enter_context(nc.sbuf_tensor("scm%d" % i, [128, 128], BF16)) for i in range(2)]
        LBT = st.enter_context(nc.sbuf_tensor("LBT", [128, 1024], F32))
        SRb = st.enter_context(nc.sbuf_tensor("SRb", [128, 256], BF16))
        flag = st.enter_context(nc.sbuf_tensor("flagt", [128, 2], F32))
        negm = st.enter_context(nc.sbuf_tensor("negm", [128, 2], F32))
        sinkt = st.enter_context(nc.sbuf_tensor("sinkt", [128, L * 8], F32))
        stat = st.enter_context(nc.sbuf_tensor("stat", [128, 64], F32))
        psb = [st.enter_context(nc.psum_tensor("ps%d" % i, [128, 512], F32)) for i in range(8)]
        P = Prog(nc)

        def PS(b):
            return ("ps", b)

        MAT = lambda i: cst[:, i * 128:(i + 1) * 128]
        M1 = [MAT(0), MAT(4)]
        M2 = [MAT(1), MAT(5)]
        M3 = [MAT(2), MAT(6)]
        UU = [MAT(3), MAT(7)]
        ONES = MAT(8)
        identb = MAT(9)
        rowt = st.enter_context(nc.sbuf_tensor("rowt", [128, 2], F32))
        ROWA = rowt[:, 0:1]
        ROWB = rowt[:, 1:2]

        X = Buf(arena, 0, 65536, F32, (NT, D), "X")
        HT = Buf(arena, 65536, 32768, BF16, (16, T), "HT")
        RING0 = 98304
        NORM0 = 180224
        GB = Buf(arena, NORM0, 8192, F32)
        XN = [Buf(arena, NORM0 + 8192 + i * 4096, 4096, BF16) for i in range(2)]
        JUNK = Buf(arena, NORM0 + 16384, 4096, BF16)

        ring_state = {"slots": None, "i": 0}

        def set_ring(n, nd=0):
            ring_state["slots"] = [Buf(arena, RING0 + i * 4096, 4096, BF16, name="ring%d" % i) for i in range(n)]
            ring_state["i"] = 0
            ring_state["dslots"] = [Buf(arena, RING0 + (n + i) * 4096, 4096, BF16, name="dring%d" % i) for i in range(nd)]
            ring_state["di"] = 0

        def wload(src_ap, ncols, dring=False):
            if dring:
                s = ring_state["dslots"][ring_state["di"] % len(ring_state["dslots"])]
                ring_state["di"] += 1
            else:
                s = ring_state["slots"][ring_state["i"] % len(ring_state["slots"])]
                ring_state["i"] += 1
            dst = s.flat[:, 0:ncols]
            P.dma("pool", lambda e: e.dma_start(out=dst, in_=src_ap), writes=s.keys, key=("ring", s.off))
            return s

        P.dma("pool", lambda e: e.dma_start(out=cst[:], in_=cst_d), writes=["cst", "identb"], key="cst")
        P.dma("sp", lambda e: e.dma_start(out=flag[:], in_=flag_d), writes=["flag"], key="flag")
        P.dma("sp", lambda e: e.dma_start(out=rowt[:], in_=cst_d[:, C_ROW:C_ROW + 2]), writes=["rowt"], key="rowt")
        P.dma("sp", lambda e: e.dma_start(out=sinkt[:], in_=sink_d), writes=["sink"], key="sink")
        P.dma("sp", lambda e: e.dma_start(out=rgnc[:], in_=rgn_d), writes=["rgnc"], key="rgnc")
        P.dma("sp", lambda e: e.dma_start(out=X.ap, in_=x_d.rearrange("(t p) d -> p t d", p=128)),
              writes=X.keys, key="xload")
        P.op("dve", lambda e: e.tensor_scalar(out=negm[:], in0=flag[:], scalar1=-1.0, scalar2=30000.0,
                                              op0=ALU.add, op1=ALU.mult), reads=["flag"], writes=["negm"])

        def norm_to_HT(gidx, out_final=False):
            P.dma("sp", lambda e: e.dma_start(out=GB.ap if False else GB.flat, in_=gn_d[gidx:gidx + 1, :].partition_broadcast(128)),
                  writes=GB.keys, key="gb")
            for t in range(NT):
                sc = stat[:, t:t + 1]
                rs = stat[:, 8 + t:9 + t]
                P.op("dve", (lambda sc: lambda e: e.memset(sc, 0.0))(sc), writes=[("stat", t)])
                P.op("act", (lambda t, sc: lambda e: e.activation(out=JUNK.flat, in_=X.ap[:, t, :], func=AF.Square,
                                                                  accum_out=sc))(t, sc),
                     reads=X.sub(t * 8192, (t + 1) * 8192) + (("stat", t),), writes=JUNK.keys + (("stat", t),))
                P.op("dve", (lambda sc, rs: lambda e: e.tensor_scalar(out=rs, in0=sc, scalar1=1.0 / D, scalar2=EPS,
                                                                      op0=ALU.mult, op1=ALU.add))(sc, rs),
                     reads=[("stat", t)], writes=[("rs", t)])
                P.op("act", (lambda rs: lambda e: e.activation(out=rs, in_=rs, func=AF.Sqrt))(rs),
                     reads=[("rs", t)], writes=[("rs", t)])
                P.op("dve", (lambda rs: lambda e: e.reciprocal(out=rs, in_=rs))(rs),
                     reads=[("rs", t)], writes=[("rs", t)])
                if out_final:
                    yb = Buf(arena, 65536 + (t % 2) * 8192, 8192, F32)
                    P.op("dve", (lambda t, rs, yb: lambda e: e.scalar_tensor_tensor(
                        out=yb.flat, in0=X.ap[:, t, :], scalar=rs, in1=GB.flat, op0=ALU.mult, op1=ALU.mult))(t, rs, yb),
                        reads=X.sub(t * 8192, (t + 1) * 8192) + GB.keys + (("rs", t),), writes=yb.keys)
                    P.dma("sp", (lambda t, yb: lambda e: e.dma_start(out=y_d[t * 128:(t + 1) * 128, :], in_=yb.flat))(t, yb),
                          reads=yb.keys, writes=[("y", t)], key="y")
                    continue
                xn = XN[t % 2]
                P.op("dve", (lambda t, rs, xn: lambda e: e.scalar_tensor_tensor(
                    out=xn.flat, in0=X.ap[:, t, :], scalar=rs, in1=GB.flat, op0=ALU.mult, op1=ALU.mult))(t, rs, xn),
                    reads=X.sub(t * 8192, (t + 1) * 8192) + GB.keys + (("rs", t),), writes=xn.keys)
                for hb in range(2):
                    b = (2 * t + hb) % 4
                    pst = psb[b][:, 0:512].bitcast(BF16)
                    for kk in range(8):
                        k = hb * 8 + kk
                        P.op("pe", (lambda pst, kk, k, xn: lambda e: e.transpose(
                            out=pst[:, kk * 128:(kk + 1) * 128], in_=xn.flat[:, k * 128:(k + 1) * 128], identity=identb))(pst, kk, k, xn),
                            reads=xn.keys + ("identb",), writes=[PS(b)])
                    dst = HT.ap[:, hb * 8:(hb + 1) * 8, t * 128:(t + 1) * 128]
                    src = pst.rearrange("p (a b) -> p a b", b=128)
                    eng = "act" if hb == 0 else "dve"
                    htk = tuple(sorted(set(("g", (65536 + k_ * 2048 + t * 256) // GRAN) for k_ in range(hb * 8, hb * 8 + 8))))
                    if eng == "act":
                        P.op("act", (lambda dst, src: lambda e: e.copy(out=dst, in_=src))(dst, src),
                             reads=[PS(b)], writes=htk)
                    else:
                        P.op("dve", (lambda dst, src: lambda e: e.tensor_copy(out=dst, in_=src))(dst, src),
                             reads=[PS(b)], writes=htk)

        def ffn(l, f):
            G = 4
            ACTB = [Buf(arena, 180224 + i * 2048, 2048, BF16) for i in range(8)]
            SIL = [Buf(arena, 196608 + i * 2048, 2048, F32) for i in range(2)]
            ngrp = nfc // G

            def down(acts, dsl):
                for t in range(NT):
                    for dc in range(4):
                        b = 4 + (t * 4 + dc) % 4
                        for ci in range(G):
                            P.op("pe", (lambda b, ci, t, dc: lambda e: e.matmul(
                                psb[b][:, :], acts[ci].flat[:, t * 128:(t + 1) * 128], dsl[ci].flat[:, dc * 512:(dc + 1) * 512],
                                start=(ci == 0), stop=(ci == G - 1)))(b, ci, t, dc),
                                reads=acts[ci].keys + dsl[ci].keys, writes=[PS(b)])
                        xs = X.ap[:, t, dc * 512:(dc + 1) * 512]
                        xk = X.sub(t * 8192 + dc * 2048, t * 8192 + (dc + 1) * 2048)
                        P.op("dve", (lambda b, xs: lambda e: e.scalar_tensor_tensor(
                            out=xs, in0=psb[b][:, :], scalar=0.5, in1=xs, op0=ALU.mult, op1=ALU.add))(b, xs),
                            reads=[PS(b)] + list(xk), writes=xk)

            pending = None
            for g in range(ngrp):
                acts = []
                dsl = []
                for ci in range(G):
                    c = g * G + ci
                    sg = wload(wgu_d[l, f, c, 0], D)
                    su = wload(wgu_d[l, f, c, 1], D)
                    ab = ACTB[(g % 2) * G + ci]
                    acts.append(ab)
                    for half in range(2):
                        bg, bu = 2 * half, 2 * half + 1
                        for (s_, b) in ((sg, bg), (su, bu)):
                            for k in range(16):
                                P.op("pe", (lambda s_, b, k, half: lambda e: e.matmul(
                                    psb[b][:, :], s_.flat[:, k * 128:(k + 1) * 128], HT.ap[:, k, half * 512:(half + 1) * 512],
                                    start=(k == 0), stop=(k == 15)))(s_, b, k, half),
                                    reads=s_.keys + HT.keys, writes=[PS(b)])
                        sl = SIL[half]
                        P.op("act", (lambda sl, bg: lambda e: e.activation(out=sl.flat, in_=psb[bg][:, :], func=AF.Silu))(sl, bg),
                             reads=[PS(bg)], writes=sl.keys)
                        P.op("dve", (lambda ab, sl, bu, half: lambda e: e.tensor_tensor(
                            out=ab.flat[:, half * 512:(half + 1) * 512], in0=psb[bu][:, :], in1=sl.flat, op=ALU.mult))(ab, sl, bu, half),
                            reads=[PS(bu)] + list(sl.keys), writes=ab.sub(half * 1024, (half + 1) * 1024))
                    dsl.append(wload(wd_d[l, f, c], D, dring=True))
                if pending is not None:
                    down(*pending)
                pending = (acts, dsl)
            down(*pending)


        def proj_fm(l, j, b0):
            s_ = wload(win_d[l, j], D)
            for half in range(2):
                for k in range(16):
                    P.op("pe", (lambda k, half: lambda e: e.matmul(
                        psb[b0 + half][:, :], s_.flat[:, k * 128:(k + 1) * 128], HT.ap[:, k, half * 512:(half + 1) * 512],
                        start=(k == 0), stop=(k == 15)))(k, half), reads=s_.keys + HT.keys, writes=[PS(b0 + half)])

        def proj_tm(l, j, b0):
            s_ = wload(win_d[l, j], D)
            for t in range(NT):
                b = b0 + t // 4
                for k in range(16):
                    P.op("pe", (lambda k, t, b: lambda e: e.matmul(
                        psb[b][:, (t % 4) * 128:(t % 4 + 1) * 128], HT.ap[:, k, t * 128:(t + 1) * 128], s_.flat[:, k * 128:(k + 1) * 128],
                        start=(k == 0), stop=(k == 15)))(k, t, b), reads=s_.keys + HT.keys, writes=[PS(b)])

        def evac2(eng, b0, dst_flat, keys, func=None, extra_reads=()):
            for half in range(2):
                d_ = dst_flat[:, half * 512:(half + 1) * 512]
                src = psb[b0 + half][:, :]
                if eng == "act":
                    f_ = func if func is not None else AF.Copy
                    P.op("act", (lambda d_, src, f_: lambda e: e.activation(out=d_, in_=src, func=f_))(d_, src, f_),
                         reads=[PS(b0 + half)] + list(extra_reads), writes=keys)
                else:
                    P.op("dve", (lambda d_, src: lambda e: e.tensor_copy(out=d_, in_=src))(d_, src),
                         reads=[PS(b0 + half)] + list(extra_reads), writes=keys)

        def mixer(l):
            set_ring(6)
            le = l + lbase
            M0 = 122880
            OAT = Buf(arena, M0, 16384, BF16, (8, T), "OAT")
            OBT = Buf(arena, M0 + 16384, 16384, BF16, (8, T), "OBT")
            KT = Buf(arena, M0 + 32768, 5120, BF16, (2, 1280), "KT")
            VV = Buf(arena, M0 + 37888, 5120, BF16, (2, 10, 128), "VV")
            H0 = M0 + 43008
            QT = Buf(arena, H0, 2048, BF16)
            Vh = Buf(arena, H0 + 2048, 2048, BF16, (8, 128))
            SG = Buf(arena, H0 + 4096, 2048, BF16, (8, 128))
            OACC = Buf(arena, H0 + 6144, 4096, F32, (8, 128))
            QBAR = [Buf(arena, H0 + 10240 + i * 2048, 2048, BF16) for i in range(2)]
            D0 = H0 + 14336
            T1 = Buf(arena, D0, 4096, F32, (8, 128))
            T2 = Buf(arena, D0 + 4096, 4096, F32, (8, 128))
            T3 = Buf(arena, D0 + 8192, 2048, BF16, (8, 128))
            KA = Buf(arena, D0 + 10240, 2048, BF16, (8, 128))
            KB = Buf(arena, D0 + 12288, 2048, BF16, (8, 128))
            KTb = Buf(arena, D0 + 14336, 2048, BF16)
            QTIL = Buf(arena, D0 + 16384, 2048, BF16)
            KTIL = Buf(arena, D0 + 18432, 2048, BF16)
            QHA = Buf(arena, D0 + 20480 - 4096 + 4096, 0, BF16) if False else None
            DEC = stat[:, 16:32]
            if do_att:
                for kv in range(2):
                    proj_fm(l, 48 + kv, 0)
                    for half in range(2):
                        d_ = KT.ap[:, kv, 128 + half * 512:128 + (half + 1) * 512]
                        P.op("act", (lambda d_, half: lambda e: e.copy(out=d_, in_=psb[half][:, :]))(d_, half),
                             reads=[PS(half)], writes=KT.keys)
                    proj_tm(l, 50 + kv, 2)
                    for hb in range(2):
                        d_ = VV.ap[:, kv, 1 + hb * 4:5 + hb * 4, :]
                        P.op("dve", (lambda d_, hb: lambda e: e.tensor_copy(
                            out=d_, in_=psb[2 + hb][:, :].rearrange("p (a b) -> p a b", b=128)))(d_, hb),
                            reads=[PS(2 + hb)], writes=VV.keys)
                n = 0
                for kv in range(2):
                    for (ksl, vt) in ((slice(128, 256), 1), (slice(1024, 1152), 8)):
                        P.dma("sp", (lambda kv, ksl, n: lambda e: e.dma_start(out=kv_in[:, n * 128:(n + 1) * 128], in_=KT.ap[:, kv, ksl]))(kv, ksl, n),
                              reads=KT.keys, writes=["kv_in"], key="kv_in")
                        P.dma("sp", (lambda kv, vt, n: lambda e: e.dma_start(out=kv_in[:, (n + 1) * 128:(n + 2) * 128], in_=VV.ap[:, kv, vt, :]))(kv, vt, n),
                              reads=VV.keys, writes=["kv_in"], key="kv_in")
                        n += 2
                P.dma("pool", lambda e: e.collective_compute("AllGather", ALU.bypass, replica_groups=[[0, 1], [2, 3], [4, 5], [6, 7]],
                                                              ins=[kv_in.opt()], outs=[kv_out.opt()]),
                      reads=["kv_in"], writes=["kv_out"], key="kv_cc", inc=1)
                for kv in range(2):
                    P.dma("sp", (lambda kv: lambda e: e.dma_start(out=KT.ap[:, kv, 0:128], in_=kv_out[0:128, (kv * 4 + 2) * 128:(kv * 4 + 3) * 128]))(kv),
                          reads=["kv_out"], writes=KT.keys, key="halo")
                    P.dma("sp", (lambda kv: lambda e: e.dma_start(out=VV.ap[:, kv, 0, :], in_=kv_out[0:128, (kv * 4 + 3) * 128:(kv * 4 + 4) * 128]))(kv),
                          reads=["kv_out"], writes=VV.keys, key="halo")
                    P.dma("sp", (lambda kv: lambda e: e.dma_start(out=KT.ap[:, kv, 1152:1280], in_=kv_out[128:256, (kv * 4 + 0) * 128:(kv * 4 + 1) * 128]))(kv),
                          reads=["kv_out"], writes=KT.keys, key="halo")
                    P.dma("sp", (lambda kv: lambda e: e.dma_start(out=VV.ap[:, kv, 9, :], in_=kv_out[128:256, (kv * 4 + 1) * 128:(kv * 4 + 2) * 128]))(kv),
                          reads=["kv_out"], writes=VV.keys, key="halo")

            def hgrn_local(h):
                par = h % 2
                proj_fm(l, h, 0)
                evac2("act", 0, QT.flat, QT.keys)
                proj_tm(l, 8 + h, 2)
                evac2("dve", 2, Vh.flat, Vh.keys)
                proj_tm(l, 32 + h, 4)
                evac2("act", 4, SG.flat, SG.keys, func=AF.Silu)
                for dr in range(2):
                    proj_tm(l, 16 + 8 * dr + h, 6)
                    evac2("act", 6, T1.flat, T1.keys, func=AF.Sigmoid)
                    P.op("dve", lambda e: e.tensor_scalar(out=T1.flat, in0=T1.flat, scalar1=-1.0, scalar2=1.0, op0=ALU.mult, op1=ALU.add),
                         reads=T1.keys, writes=T1.keys)
                    if le > 0:
                        P.dma("sp", (lambda dr: lambda e: e.dma_start(out=LBT[:], in_=lbl_d[dr, le, h:h + 1, :].partition_broadcast(128)))(dr),
                              writes=["LBT"], key="lbt")
                        P.dma("sp", (lambda dr: lambda e: e.dma_start(out=T2.flat, in_=lbl_d[dr, 0, h:h + 1, :].partition_broadcast(128)))(dr),
                              writes=T2.keys, key="lbt2")
                        P.op("dve", lambda e: e.tensor_tensor(out=LBT[:], in0=LBT[:], in1=T2.flat, op=ALU.subtract),
                             reads=("LBT",) + T2.keys, writes=["LBT"])
                        P.op("act", lambda e: e.activation(out=LBT[:], in_=LBT[:], func=AF.Sigmoid), reads=["LBT"], writes=["LBT"])
                        P.op("dve", lambda e: e.tensor_tensor(out=T2.flat, in0=LBT[:], in1=T1.flat, op=ALU.mult),
                             reads=("LBT",) + T1.keys, writes=T2.keys)
                        P.op("dve", lambda e: e.tensor_tensor(out=T1.flat, in0=T1.flat, in1=T2.flat, op=ALU.subtract),
                             reads=T1.keys + T2.keys, writes=T1.keys)
                    P.op("dve", lambda e: e.tensor_copy(out=KA.flat, in_=T1.flat), reads=T1.keys, writes=KA.keys)
                    P.op("dve", lambda e: e.tensor_scalar(out=T2.flat, in0=T1.flat, scalar1=-1.0, scalar2=1.0, op0=ALU.mult, op1=ALU.add),
                         reads=T1.keys, writes=T2.keys)
                    P.op("act", lambda e: e.activation(out=T2.flat, in_=T2.flat, func=AF.Ln), reads=T2.keys, writes=T2.keys)
                    AH = Buf(arena, T2.off, 2048, BF16, (8, 128))
                    AM = Buf(arena, T2.off + 2048, 2048, BF16, (8, 128))
                    P.op("dve", lambda e: e.tensor_copy(out=T3.flat, in_=T2.flat), reads=T2.keys, writes=T3.keys)
                    P.op("dve", lambda e: e.tensor_tensor(out=T2.flat, in0=T2.flat, in1=T3.flat, op=ALU.subtract), reads=T2.keys + T3.keys, writes=T2.keys)
                    P.op("dve", lambda e: e.tensor_copy(out=QTIL.flat, in_=T2.flat), reads=T2.keys, writes=QTIL.keys)
                    P.op("dve", lambda e: e.tensor_copy(out=AH.flat, in_=T3.flat), reads=T3.keys + T2.keys, writes=AH.keys)
                    P.op("dve", lambda e: e.tensor_copy(out=AM.flat, in_=QTIL.flat), reads=QTIL.keys + T2.keys, writes=AM.keys)
                    APC = (AH, AM)
                    pst = psb[0][:, :].bitcast(BF16)
                    for t in range(NT):
                        P.op("pe", (lambda t: lambda e: e.transpose(out=pst[:, t * 128:(t + 1) * 128], in_=KA.ap[:, t, :], identity=identb))(t),
                             reads=KA.keys + ("identb",), writes=[PS(0)])
                    P.op("dve", lambda e: e.tensor_copy(out=KTb.flat, in_=pst), reads=[PS(0)], writes=KTb.keys)

                    def emat(b0, rhs_m, tm=False):
                        for t in range(NT):
                            b = b0 + t // 4
                            o_ = psb[b][:, (t % 4) * 128:(t % 4 + 1) * 128]
                            for pi, ap_ in enumerate(APC):
                                if tm:
                                    P.op("pe", (lambda t, o_, pi, ap_: lambda e: e.matmul(o_, rhs_m, ap_.ap[:, t, :], start=(pi == 0), stop=(pi == 1)))(t, o_, pi, ap_),
                                         reads=T2.keys + ("cst",), writes=[PS(b)])
                                else:
                                    P.op("pe", (lambda t, o_, pi, ap_: lambda e: e.matmul(o_, ap_.ap[:, t, :], rhs_m, start=(pi == 0), stop=(pi == 1)))(t, o_, pi, ap_),
                                         reads=T2.keys + ("cst",), writes=[PS(b)])

                    emat(2, M1[dr])
                    for half in range(2):
                        sl_ = slice(half * 512, (half + 1) * 512)
                        P.op("act", (lambda half, sl_: lambda e: e.activation(out=T3.flat[:, sl_], in_=psb[2 + half][:, :], func=AF.Exp))(half, sl_),
                             reads=[PS(2 + half)], writes=T3.keys)
                        P.op("dve", (lambda sl_: lambda e: e.tensor_tensor(out=QTIL.flat[:, sl_], in0=QT.flat[:, sl_], in1=T3.flat[:, sl_], op=ALU.mult))(sl_),
                             reads=QT.keys + T3.keys, writes=QTIL.keys)
                        P.op("act", (lambda half, sl_: lambda e: e.activation(out=T3.flat[:, sl_], in_=psb[2 + half][:, :], func=AF.Exp, scale=-1.0))(half, sl_),
                             reads=[PS(2 + half)], writes=T3.keys)
                        P.op("dve", (lambda sl_: lambda e: e.tensor_tensor(out=KTIL.flat[:, sl_], in0=KTb.flat[:, sl_], in1=T3.flat[:, sl_], op=ALU.mult))(sl_),
                             reads=KTb.keys + T3.keys, writes=KTIL.keys)
                    for t in range(NT):
                        b = 4 + t // 4
                        o_ = psb[b][:, (t % 4) * 128:(t % 4 + 1) * 128]
                        rs_ = list(range(0, t + 1)) if dr == 0 else list(range(t, NT))
                        for i_, r_ in enumerate(rs_):
                            rm = UU[dr] if r_ == t else ONES
                            for pi, ap_ in enumerate(APC):
                                P.op("pe", (lambda o_, r_, rm, i_, n_, pi, ap_: lambda e: e.matmul(o_, ap_.ap[:, r_, :], rm, start=(i_ == 0 and pi == 0), stop=(i_ == n_ - 1 and pi == 1)))(o_, r_, rm, i_, len(rs_), pi, ap_),
                                     reads=T2.keys + ("cst",), writes=[PS(b)])
                    for half in range(2):
                        sl_ = slice(half * 512, (half + 1) * 512)
                        P.op("act", (lambda half, sl_: lambda e: e.activation(out=T3.flat[:, sl_], in_=psb[4 + half][:, :], func=AF.Exp))(half, sl_),
                             reads=[PS(4 + half)], writes=T3.keys)
                        P.op("dve", (lambda sl_, dr: lambda e: e.tensor_tensor(out=QBAR[dr].flat[:, sl_], in0=QT.flat[:, sl_], in1=T3.flat[:, sl_], op=ALU.mult))(sl_, dr),
                             reads=QT.keys + T3.keys, writes=QBAR[dr].keys)
                    emat(6, M3[dr], tm=True)
                    for half in range(2):
                        sl_ = slice(half * 512, (half + 1) * 512)
                        P.op("act", (lambda half, sl_: lambda e: e.activation(out=T3.flat[:, sl_], in_=psb[6 + half][:, :], func=AF.Exp))(half, sl_),
                             reads=[PS(6 + half)], writes=T3.keys)
                    P.op("dve", lambda e: e.scalar_tensor_tensor(out=KA.flat, in0=T3.flat, scalar=ROWA, in1=T1.flat, op0=ALU.mult, op1=ALU.mult),
                         reads=T3.keys + T1.keys + ("rowt",), writes=KA.keys)
                    P.op("dve", lambda e: e.scalar_tensor_tensor(out=KB.flat, in0=T3.flat, scalar=ROWB, in1=T1.flat, op0=ALU.mult, op1=ALU.mult),
                         reads=T3.keys + T1.keys + ("rowt",), writes=KB.keys)
                    emat(2, M2[dr])
                    QH = [Buf(arena, T1.off, 2048, BF16), Buf(arena, T1.off + 2048, 2048, BF16)]
                    ecol = 63 if dr == 0 else 0
                    for half in range(2):
                        sl_ = slice(half * 512, (half + 1) * 512)
                        P.op("act", (lambda half, ecol: lambda e: e.activation(
                            out=DEC[:, half * 8:(half + 1) * 8], in_=psb[2 + half][:, :].rearrange("p (c j) -> p c j", j=64)[:, :, ecol], func=AF.Exp))(half, ecol),
                            reads=[PS(2 + half)], writes=["dec"])
                        P.op("act", (lambda half, sl_: lambda e: e.activation(out=T3.flat[:, sl_], in_=psb[2 + half][:, :], func=AF.Exp))(half, sl_),
                             reads=[PS(2 + half)], writes=T3.keys)
                    for cc in range(2):
                        P.op("pool", (lambda cc: lambda e: e.memset(QH[cc].flat, 0.0))(cc), reads=KA.keys + KB.keys, writes=QH[cc].keys)
                        v4 = (lambda cc_: (lambda ap_: ap_.rearrange("p (t c j) -> p t c j", c=2, j=64)[:, :, cc_, :]))(cc)
                        P.op("dve", (lambda cc, v4: lambda e: e.tensor_tensor(out=v4(QH[cc].flat), in0=v4(QT.flat), in1=v4(T3.flat), op=ALU.mult))(cc, v4),
                             reads=QT.keys + T3.keys, writes=QH[cc].keys)
                    KH = [KA, KB]
                    tiles = list(range(NT)) if dr == 0 else list(range(NT - 1, -1, -1))
                    corder = (0, 1) if dr == 0 else (1, 0)
                    SbAll = LBT[:].bitcast(BF16)
                    stp = lambda c: psb[c // 4][:, (c % 4) * 128:(c % 4 + 1) * 128]
                    for t in range(NT):
                        for cc in range(2):
                            c = 2 * t + cc
                            P.op("pe", (lambda cc, t, c: lambda e: e.matmul(stp(c), KH[cc].ap[:, t, :], Vh.ap[:, t, :], start=True, stop=True))(cc, t, c),
                                 reads=KH[cc].keys + Vh.keys, writes=[PS(c // 4)])
                    P.op("dve", lambda e: e.memset(Sf[:], 0.0), writes=["Sf"])
                    for t in tiles:
                        for cc in corder:
                            c = 2 * t + cc
                            P.op("dve", (lambda c: lambda e: e.tensor_copy(out=SbAll[:, c * 128:(c + 1) * 128], in_=Sf[:]))(c),
                                 reads=["Sf"], writes=["LBT"])
                            P.op("dve", (lambda c: lambda e: e.scalar_tensor_tensor(out=Sf[:], in0=Sf[:], scalar=DEC[:, c:c + 1], in1=stp(c), op0=ALU.mult, op1=ALU.add))(c),
                                 reads=["Sf", "dec", PS(c // 4)], writes=["Sf"])
                    for it, t in enumerate(tiles):
                        tsl = slice(t * 128, (t + 1) * 128)
                        sbk = 6 + it % 2
                        sc_ps = psb[sbk][:, 0:128]
                        sck = PS(sbk)
                        P.op("pe", (lambda tsl, sc_ps: lambda e: e.matmul(sc_ps, KTIL.flat[:, tsl], QTIL.flat[:, tsl], start=True, stop=True))(tsl, sc_ps),
                             reads=KTIL.keys + QTIL.keys, writes=[sck])
                        sm = scm[it % 2]
                        smk = ("scm", it % 2)
                        P.op("dve", (lambda sm, sc_ps, dr: lambda e: e.tensor_tensor(out=sm[:], in0=sc_ps, in1=M2[dr], op=ALU.mult))(sm, sc_ps, dr),
                             reads=[sck, "cst"], writes=[smk])
                        ob = 4 + it % 2
                        o_ps = psb[ob][:, 0:128]
                        P.op("pe", (lambda sm, t, o_ps: lambda e: e.matmul(o_ps, sm[:], Vh.ap[:, t, :], start=True, stop=False))(sm, t, o_ps),
                             reads=[smk] + list(Vh.keys), writes=[PS(ob)])
                        for ci, cc in enumerate(corder):
                            c = 2 * t + cc
                            P.op("pe", (lambda cc, tsl, o_ps, ci, c: lambda e: e.matmul(o_ps, QH[cc].flat[:, tsl], SbAll[:, c * 128:(c + 1) * 128], start=False, stop=(ci == 1)))(cc, tsl, o_ps, ci, c),
                                 reads=QH[cc].keys + ("LBT",), writes=[PS(ob)])
                        if dr == 0:
                            P.op("act", (lambda t, o_ps: lambda e: e.copy(out=OACC.ap[:, t, :], in_=o_ps))(t, o_ps),
                                 reads=[PS(ob)], writes=OACC.sub(t * 512, (t + 1) * 512))
                        else:
                            P.op("dve", (lambda t, o_ps: lambda e: e.tensor_tensor(out=OACC.ap[:, t, :], in0=OACC.ap[:, t, :], in1=o_ps, op=ALU.add))(t, o_ps),
                                 reads=[PS(ob)] + list(OACC.sub(t * 512, (t + 1) * 512)), writes=OACC.sub(t * 512, (t + 1) * 512))
                    P.dma("sp", (lambda dr, par: lambda e: e.dma_start(out=st_in[par][:, dr * 128:(dr + 1) * 128], in_=Sf[:]))(dr, par),
                          reads=["Sf"], writes=[("st_in", par)], key=("st_in", par))
                P.dma("pool", (lambda par: lambda e: e.collective_compute("AllGather", ALU.bypass, replica_groups=[[0, 1], [2, 3], [4, 5], [6, 7]],
                                                                          ins=[st_in[par].opt()], outs=[st_out[par].opt()]))(par),
                      reads=[("st_in", par)], writes=[("st_out", par)], key=("st_cc", par), inc=1)

            def hgrn_final(h):
                par = h % 2
                SR = Buf(arena, T1.off, 1024, F32)
                P.dma("sp", (lambda par: lambda e: e.dma_start(out=SR.flat[:, 0:128], in_=st_out[par][0:128, 0:128]))(par),
                      reads=[("st_out", par)], writes=SR.keys, key="sr")
                P.dma("sp", (lambda par: lambda e: e.dma_start(out=SR.flat[:, 128:256], in_=st_out[par][128:256, 128:256]))(par),
                      reads=[("st_out", par)], writes=SR.keys, key="sr")
                for dr in range(2):
                    P.op("dve", (lambda dr: lambda e: e.tensor_scalar(out=SRb[:, dr * 128:(dr + 1) * 128], in0=SR.flat[:, dr * 128:(dr + 1) * 128],
                                                                      scalar1=flag[:, dr:dr + 1], scalar2=None, op0=ALU.mult))(dr),
                         reads=SR.keys + ("flag",), writes=["SRb"])
                for t in range(NT):
                    tsl = slice(t * 128, (t + 1) * 128)
                    ob = 4 + t % 2
                    o_ps = psb[ob][:, 0:128]
                    for dr in range(2):
                        P.op("pe", (lambda dr, tsl, o_ps: lambda e: e.matmul(o_ps, QBAR[dr].flat[:, tsl], SRb[:, dr * 128:(dr + 1) * 128], start=(dr == 0), stop=(dr == 1)))(dr, tsl, o_ps),
                             reads=QBAR[dr].keys + ("SRb",), writes=[PS(ob)])
                    P.op("dve", (lambda t, o_ps: lambda e: e.tensor_tensor(out=OACC.ap[:, t, :], in0=OACC.ap[:, t, :], in1=o_ps, op=ALU.add))(t, o_ps),
                         reads=[PS(ob)] + list(OACC.sub(t * 512, (t + 1) * 512)), writes=OACC.sub(t * 512, (t + 1) * 512))
                P.op("dve", lambda e: e.tensor_tensor(out=T2.flat, in0=OACC.flat, in1=OACC.flat, op=ALU.mult), reads=OACC.keys, writes=T2.keys)
                ssq = stat[:, 32:40]
                P.op("dve", lambda e: e.tensor_reduce(out=ssq, in_=T2.ap, axis=AX.X, op=ALU.add), reads=T2.keys, writes=["ssq"])
                P.op("dve", lambda e: e.tensor_scalar(out=ssq, in0=ssq, scalar1=1.0 / 128, scalar2=EPS, op0=ALU.mult, op1=ALU.add), reads=["ssq"], writes=["ssq"])
                P.op("act", lambda e: e.activation(out=ssq, in_=ssq, func=AF.Sqrt), reads=["ssq"], writes=["ssq"])
                P.op("dve", lambda e: e.reciprocal(out=ssq, in_=ssq), reads=["ssq"], writes=["ssq"])
                for t in range(NT):
                    P.op("dve", (lambda t: lambda e: e.scalar_tensor_tensor(out=KA.ap[:, t, :], in0=OACC.ap[:, t, :], scalar=ssq[:, t:t + 1], in1=SG.ap[:, t, :],
                                                                          op0=ALU.mult, op1=ALU.mult))(t),
                         reads=OACC.keys + SG.keys + ("ssq",), writes=KA.keys)
                pst = psb[0][:, :].bitcast(BF16)
                for t in range(NT):
                    P.op("pe", (lambda t: lambda e: e.transpose(out=pst[:, t * 128:(t + 1) * 128], in_=KA.ap[:, t, :], identity=identb))(t),
                         reads=KA.keys + ("identb",), writes=[PS(0)])
                P.op("dve", (lambda h: lambda e: e.tensor_scalar(out=OAT.ap[:, h, :], in0=pst, scalar1=rgnc[:, le * 8 + h:le * 8 + h + 1], scalar2=None, op0=ALU.mult))(h),
                     reads=[PS(0), "rgnc"], writes=OAT.sub(h * 2048, (h + 1) * 2048))

            if do_hgrn:
                for h in range(8):
                    hgrn_local(h)
                    hgrn_final(h)
            else:
                P.op("dve", lambda e: e.memset(OAT.flat, 0.0), writes=OAT.keys)

            if do_att:
                AB = Buf(arena, D0, 12288, F32, (8, 384), "AB")
                QA = Buf(arena, D0 + 12288, 2048, BF16)
                SC = [Buf(arena, D0 + 14336 + i * 1536, 1536, F32) for i in range(2)]
                PX = [Buf(arena, D0 + 17408 + i * 768, 768, BF16) for i in range(2)]
                PTs = [Buf(arena, D0 + 18944 + i * 768, 768, BF16) for i in range(2)]
                ON = [Buf(arena, H0 + i * 256, 256, BF16) for i in range(2)]
                P.dma("sp", lambda e: e.dma_start(out=AB.flat, in_=abias_d), writes=AB.keys, key="ab")
                isq = 1.0 / math.sqrt(128.0)
                n = 0
                for a in range(8):
                    kv = a // 4
                    proj_fm(l, 40 + a, 0)
                    evac2("act", 0, QA.flat, QA.keys)
                    for i in range(NT):
                        sc, px, pt, on = SC[n % 2], PX[n % 2], PTs[n % 2], ON[n % 2]
                        sb_ = 2 + n % 2
                        st8 = stat[:, 40 + (n % 2) * 8:48 + (n % 2) * 8]
                        stk = ("st8", n % 2)
                        mx, nmx, rsum, sterm = st8[:, 0:1], st8[:, 1:2], st8[:, 2:3], st8[:, 3:4]
                        n += 1
                        P.op("pe", (lambda i, kv, sb_: lambda e: e.matmul(psb[sb_][:, 0:384], QA.flat[:, i * 128:(i + 1) * 128], KT.ap[:, kv, i * 128:i * 128 + 384],
                                                                      start=True, stop=True))(i, kv, sb_), reads=QA.keys + KT.keys, writes=[PS(sb_)])
                        P.op("dve", (lambda sc, a, sb_: lambda e: e.scalar_tensor_tensor(out=sc.flat, in0=psb[sb_][:, 0:384], scalar=isq, in1=AB.ap[:, a, :],
                                                                                     op0=ALU.mult, op1=ALU.add))(sc, a, sb_),
                             reads=[PS(sb_)] + list(AB.keys), writes=sc.keys)
                        if i == 0:
                            P.op("dve", (lambda sc: lambda e: e.tensor_scalar(out=sc.flat[:, 0:128], in0=sc.flat[:, 0:128], scalar1=negm[:, 0:1], scalar2=None, op0=ALU.add))(sc),
                                 reads=sc.keys + ("negm",), writes=sc.keys)
                        if i == NT - 1:
                            P.op("dve", (lambda sc: lambda e: e.tensor_scalar(out=sc.flat[:, 256:384], in0=sc.flat[:, 256:384], scalar1=negm[:, 1:2], scalar2=None, op0=ALU.add))(sc),
                                 reads=sc.keys + ("negm",), writes=sc.keys)
                        P.op("dve", (lambda sc, mx: lambda e: e.reduce_max(out=mx, in_=sc.flat, axis=AX.X))(sc, mx), reads=sc.keys, writes=[stk])
                        P.op("dve", (lambda mx, nmx: lambda e: e.tensor_scalar(out=nmx, in0=mx, scalar1=-1.0, scalar2=None, op0=ALU.mult))(mx, nmx), reads=[stk], writes=[stk])
                        P.op("dve", (lambda rsum: lambda e: e.memset(rsum, 0.0))(rsum), reads=[stk], writes=[stk])
                        P.op("act", (lambda sc, px, nmx, rsum: lambda e: e.activation(out=px.flat, in_=sc.flat, func=AF.Exp, bias=nmx, accum_out=rsum))(sc, px, nmx, rsum),
                             reads=sc.keys + (stk,), writes=px.keys + (stk,))
                        P.op("act", (lambda mx, sterm, a: lambda e: e.activation(out=sterm, in_=mx, func=AF.Exp, scale=-1.0, bias=sinkt[:, le * 8 + a:le * 8 + a + 1]))(mx, sterm, a),
                             reads=[stk, "sink"], writes=[stk])
                        P.op("dve", (lambda rsum, sterm: lambda e: e.tensor_tensor(out=rsum, in0=rsum, in1=sterm, op=ALU.add))(rsum, sterm), reads=[stk], writes=[stk])
                        P.op("dve", (lambda rsum: lambda e: e.reciprocal(out=rsum, in_=rsum))(rsum), reads=[stk], writes=[stk])
                        tb = 4 + (n % 2)
                        ptp = psb[tb][:, 0:192].bitcast(BF16)
                        for j in range(3):
                            P.op("pe", (lambda j, px, ptp: lambda e: e.transpose(out=ptp[:, j * 128:(j + 1) * 128], in_=px.flat[:, j * 128:(j + 1) * 128], identity=identb))(j, px, ptp),
                                 reads=px.keys + ("identb",), writes=[PS(tb)])
                        P.op("dve", (lambda pt, ptp: lambda e: e.tensor_copy(out=pt.flat, in_=ptp))(pt, ptp), reads=[PS(tb)], writes=pt.keys)
                        ob = 6 + (n % 2)
                        o_ps = psb[ob][:, 0:128]
                        for j in range(3):
                            P.op("pe", (lambda j, pt, o_ps, kv, i: lambda e: e.matmul(o_ps, pt.flat[:, j * 128:(j + 1) * 128], VV.ap[:, kv, i + j, :], start=(j == 0), stop=(j == 2)))(j, pt, o_ps, kv, i),
                                 reads=pt.keys + VV.keys, writes=[PS(ob)])
                        P.op("dve", (lambda on, o_ps, rsum: lambda e: e.tensor_scalar(out=on.flat, in0=o_ps, scalar1=rsum, scalar2=None, op0=ALU.mult))(on, o_ps, rsum),
                             reads=[PS(ob), stk], writes=on.keys)
                        tb2 = n % 2
                        otp = psb[tb2][:, 0:64].bitcast(BF16)
                        P.op("pe", (lambda on, otp: lambda e: e.transpose(out=otp, in_=on.flat, identity=identb))(on, otp),
                             reads=on.keys + ("identb",), writes=[PS(tb2)])
                        P.op("act", (lambda a, i, otp: lambda e: e.copy(out=OBT.ap[:, a, i * 128:(i + 1) * 128], in_=otp))(a, i, otp),
                             reads=[PS(tb2)], writes=OBT.sub(a * 2048 + i * 256, a * 2048 + (i + 1) * 256))
            else:
                P.op("dve", lambda e: e.memset(OBT.flat, 0.0), writes=OBT.keys)

            if dbg and l == 0:
                P.dma("sp", lambda e: e.dma_start(out=dbg_d[0].rearrange("h p t -> p h t"), in_=OAT.ap), reads=OAT.keys, writes=["dbg0"], key="dbg")
                P.dma("sp", lambda e: e.dma_start(out=dbg_d[1].rearrange("h p t -> p h t"), in_=OBT.ap), reads=OBT.keys, writes=["dbg1"], key="dbg")
            MT = Buf(arena, M0 + 32768, 32768, BF16, (16, T), "MT")
            G1 = Buf(arena, M0 + 65536, 4096, F32)
            G2 = Buf(arena, M0 + 69632, 4096, F32)
            G3 = Buf(arena, M0 + 73728, 4096, F32)
            for j in range(16):
                proj_fm(l, 52 + j, 0)
                evac2("act", 0, G1.flat, G1.keys, func=AF.Sigmoid)
                for (w_i, src, b0) in ((0, OAT, 2), (1, OBT, 6)):
                    s_ = wload(wab_d[l, w_i, j], RGW)
                    for half in range(2):
                        for k in range(8):
                            P.op("pe", (lambda s_, src, b0, k, half: lambda e: e.matmul(
                                psb[b0 + half][:, :], s_.flat[:, k * 128:(k + 1) * 128], src.ap[:, k, half * 512:(half + 1) * 512],
                                start=(k == 0), stop=(k == 7)))(s_, src, b0, k, half), reads=s_.keys + src.keys, writes=[PS(b0 + half)])
                proj_fm(l, 68 + j, 4)
                evac2("act", 4, G2.flat, G2.keys, func=AF.Sigmoid)
                for half in range(2):
                    sl_ = slice(half * 512, (half + 1) * 512)
                    P.op("dve", (lambda half, sl_: lambda e: e.tensor_tensor(out=G1.flat[:, sl_], in0=psb[2 + half][:, :], in1=G1.flat[:, sl_], op=ALU.mult))(half, sl_),
                         reads=[PS(2 + half)] + list(G1.keys), writes=G1.keys)
                    P.op("dve", (lambda half, sl_: lambda e: e.tensor_tensor(out=G2.flat[:, sl_], in0=psb[6 + half][:, :], in1=G2.flat[:, sl_], op=ALU.mult))(half, sl_),
                         reads=[PS(6 + half)] + list(G2.keys), writes=G2.keys)
                P.op("pool", (lambda j: lambda e: e.tensor_tensor(out=MT.ap[:, j, :], in0=G1.flat, in1=G2.flat, op=ALU.add))(j),
                     reads=G1.keys + G2.keys, writes=MT.sub(j * 2048, (j + 1) * 2048))
            for dc in range(4):
                ws = [wload(wout_d[l, dc, jg], D) for jg in range(4)]
                for t in range(NT):
                    b = (dc * NT + t) % 4
                    for j in range(16):
                        P.op("pe", (lambda b, j, t, ws=ws: lambda e: e.matmul(
                            psb[b][:, :], MT.ap[:, j, t * 128:(t + 1) * 128], ws[j // 4].flat[:, (j % 4) * 512:(j % 4 + 1) * 512],
                            start=(j == 0), stop=(j == 15)))(b, j, t), reads=MT.sub(j * 2048, (j + 1) * 2048) + ws[j // 4].keys, writes=[PS(b)])
                    xs = X.ap[:, t, dc * 512:(dc + 1) * 512]
                    xk = X.sub(t * 8192 + dc * 2048, t * 8192 + (dc + 1) * 2048)
                    P.op("dve", (lambda b, xs: lambda e: e.tensor_tensor(out=xs, in0=psb[b][:, :], in1=xs, op=ALU.add))(b, xs),
                         reads=[PS(b)] + list(xk), writes=xk)

        for l in range(nl):
            if do_ffn:
                set_ring(12, 8)
                norm_to_HT(3 * (l + lbase) + 0)
                ffn(l, 0)
            if do_mix:
                norm_to_HT(3 * (l + lbase) + 1)
                mixer(l)
            if do_ffn:
                set_ring(12, 8)
                norm_to_HT(3 * (l + lbase) + 2)
                ffn(l, 1)
        norm_to_HT(3 * L, out_final=True)
        P.emit(final_dma_keys=["y", "dbg"] if (dbg and do_mix) else ["y"])
    return nc


def _tile_cols(w, kc):
    ncol = w.shape[1]
    a = w.reshape(kc, 128, ncol // 128, 128)
    return np.ascontiguousarray(a.transpose(2, 1, 0, 3)).reshape(ncol // 128, 128, kc * 128)


def prep_inputs(x, ffn1_norm, ffn1_w_gate, ffn1_w_up, ffn1_w_down, mix_norm, w_in, lb_fwd_logits,
                lb_bwd_logits, rg_out_norm, attn_sink, w_branch_a, w_branch_b, w_out, ffn2_norm,
                ffn2_w_gate, ffn2_w_up, ffn2_w_down, final_norm, nl=L, nfc=NFC, do_mix=True, lbase=0):
    f32 = lambda a: np.asarray(a, dtype=np.float32)
    wgu = np.empty((nl, 2, nfc, 2, 128, D), np.float32)
    wd = np.empty((nl, 2, nfc, 128, D), np.float32)
    for l0 in range(nl):
        l = l0
        for f, (g, u, d) in enumerate(((ffn1_w_gate, ffn1_w_up, ffn1_w_down), (ffn2_w_gate, ffn2_w_up, ffn2_w_down))):
            wgu[l0, f, :, 0] = _tile_cols(f32(g[l + lbase])[:, :nfc * 128], 16)
            wgu[l0, f, :, 1] = _tile_cols(f32(u[l + lbase])[:, :nfc * 128], 16)
            wd[l0, f] = f32(d[l + lbase])[:nfc * 128].reshape(nfc, 128, D)
    if do_mix:
        win = np.stack([_tile_cols(f32(w_in[l + lbase]), 16) for l in range(nl)])
        wab = np.stack([np.stack([_tile_cols(f32(w_branch_a[l + lbase]), 8), _tile_cols(f32(w_branch_b[l + lbase]), 8)]) for l in range(nl)])
        wout = np.stack([np.ascontiguousarray(f32(w_out[l + lbase]).reshape(4, 4, 128, 4, 512).transpose(3, 0, 2, 1, 4)).reshape(4, 4, 128, D) for l in range(nl)])
    else:
        win = np.zeros((1, 1, 128, D), np.float32)
        wab = np.zeros((1, 2, 1, 128, RGW), np.float32)
        wout = np.zeros((1, 1, 1, 128, D), np.float32)
    gn = np.concatenate([np.stack([f32(ffn1_norm[l]), f32(mix_norm[l]), f32(ffn2_norm[l])]) for l in range(L)]
                        + [f32(final_norm)[None]], axis=0)
    lbl = np.stack([f32(lb_fwd_logits), f32(lb_bwd_logits)]).reshape(2, L, 8, 1, 128)
    lbl = np.ascontiguousarray(np.broadcast_to(lbl, (2, L, 8, 8, 128))).reshape(2, L, 8, 1024)
    rgn = np.ascontiguousarray(f32(rg_out_norm).reshape(L * 8, 128).T)
    sink = np.ascontiguousarray(np.broadcast_to(f32(attn_sink).reshape(1, L * 8), (128, L * 8)))
    cst, abias = _consts()
    xs = f32(x).reshape(8, T, D)
    maps = []
    for c in range(8):
        fl = np.zeros((128, 2), np.float32)
        fl[:, 0] = c % 2
        fl[:, 1] = 1 - c % 2
        maps.append({"x": xs[c], "wgu": wgu, "wd": wd, "win": win, "wab": wab, "wout": wout, "gn": gn,
                     "lbl": lbl, "rgn": rgn, "sink": sink, "cst": cst, "flag": fl, "abias": abias})
    return maps


_NC_CACHE = {}


def kernel(**inputs):
    maps = prep_inputs(**inputs)
    if "nc" not in _NC_CACHE:
        _NC_CACHE["nc"] = build()
    res = run_bass_kernel_spmd(_NC_CACHE["nc"], maps, core_ids=list(range(8)))
    y = np.stack([np.asarray(r["y"], dtype=np.float32) for r in res.results])
    return y.reshape(4, 2048, D)
```
